# Optimizing a Trainium2 kernel written in Bass

```python
import math
import jax
import jax.numpy as jnp
from jax import lax
import numpy as np

D_MODEL = 2048
BATCH = 1
SEQ = 8192
DEPTH = 2

EPS = 1e-6
D_FF = 5632
N_EVEN = (DEPTH + 1) // 2
N_ODD = DEPTH // 2

A_HEADS = 8
A_HEAD_DIM = 128
A_WIDTH = A_HEADS * A_HEAD_DIM
A_CHUNK = 128
B_HEADS = 4
B_DK = 128
B_DV = 256
B_KEY = B_HEADS * B_DK
B_VAL = B_HEADS * B_DV
B_GATE_RANK = 16
B_GATE_TAU = 16.0
B_CHUNK = 64
EVEN_SPLITS = (2 * A_WIDTH, 2 * A_WIDTH + B_KEY, 2 * A_WIDTH + 2 * B_KEY,
               2 * A_WIDTH + 2 * B_KEY + B_VAL, 2 * A_WIDTH + 2 * B_KEY + 2 * B_VAL)
EVEN_IN = 2 * A_WIDTH + 2 * B_KEY + 2 * B_VAL + B_GATE_RANK
EVEN_MIX = A_WIDTH + B_VAL
C_WIDTH = 1024
C_GROUP = 16
C_GROUPS = C_WIDTH // C_GROUP
C_STATE = 64

kernel_name = 'hybrid_gmlp_gla_s5_macaron'


def rmsnorm(x, g):
    xf = x.astype(jnp.float32)
    y = xf * lax.rsqrt(jnp.mean(xf * xf, axis=-1, keepdims=True) + EPS) * g.astype(jnp.float32)
    return y.astype(x.dtype)


def layernorm(x, g, b):
    xf = x.astype(jnp.float32)
    mu = jnp.mean(xf, axis=-1, keepdims=True)
    xc = xf - mu
    y = xc * lax.rsqrt(jnp.mean(xc * xc, axis=-1, keepdims=True) + EPS)
    return (y * g.astype(jnp.float32) + b.astype(jnp.float32)).astype(x.dtype)


def swiglu(x, w_gate, w_up, w_down):
    return (jax.nn.silu(x @ w_gate) * (x @ w_up)) @ w_down


def spatial_gating(z, ln_g, ln_b, w_s, b_s):
    u, v = jnp.split(z, 2, axis=-1)
    v = layernorm(v, ln_g, ln_b)
    bsz, seq, _ = v.shape
    nc = seq // A_CHUNK
    v = v.reshape(bsz, nc, A_CHUNK, A_HEADS, A_HEAD_DIM)
    causal = jnp.tril(jnp.ones((A_CHUNK, A_CHUNK), dtype=bool))
    w = jnp.where(causal, w_s, jnp.zeros_like(w_s))
    s = jnp.einsum('hts,bcshd->bcthd', w, v) + b_s.T[:, :, None]
    return u * s.reshape(bsz, seq, A_WIDTH)


def gla(q, k, v, gk, r, norm_g):
    f32 = jnp.float32
    bsz, seq, _ = q.shape
    nc = seq // B_CHUNK

    def chunks(t, d):
        return t.astype(f32).reshape(bsz, nc, B_CHUNK, B_HEADS, d)

    q = chunks(q, B_DK) * (B_DK ** -0.5)
    k = chunks(k, B_DK)
    v = chunks(v, B_DV)
    g = chunks(gk, B_DK)
    bcum = jnp.cumsum(g, axis=2)
    b_last = bcum[:, :, -1]
    q_dec = q * jnp.exp(bcum)
    k_inv = k * jnp.exp(-bcum)
    k_end = k * jnp.exp(b_last[:, :, None] - bcum)
    causal = jnp.tril(jnp.ones((B_CHUNK, B_CHUNK), dtype=bool))
    scores = jnp.einsum('bcihk,bcjhk->bchij', q_dec, k_inv)
    scores = jnp.where(causal, scores, jnp.zeros_like(scores))
    o_intra = jnp.einsum('bchij,bcjhv->bcihv', scores, v)
    ds = jnp.einsum('bcjhk,bcjhv->bchkv', k_end, v)
    decay = jnp.exp(b_last)

    def step(state, inp):
        d, d_s = inp
        return d[..., None] * state + d_s, state

    init = jnp.zeros((bsz, B_HEADS, B_DK, B_DV), f32)
    _, s_prev = lax.scan(step, init, (jnp.moveaxis(decay, 1, 0), jnp.moveaxis(ds, 1, 0)))
    s_prev = jnp.moveaxis(s_prev, 0, 1)
    o = o_intra + jnp.einsum('bcihk,bchkv->bcihv', q_dec, s_prev)
    o = o * lax.rsqrt(jnp.mean(o * o, axis=-1, keepdims=True) + EPS) * norm_g.astype(f32)
    o = o.reshape(bsz, seq, B_VAL)
    return (o * jax.nn.silu(r.astype(f32))).astype(r.dtype)


def s5_ssm(u, lam_re, lam_im, log_dt, b_re, b_im, c_re, c_im, d_skip):
    f32 = jnp.float32
    bsz, seq, _ = u.shape
    uf = u.astype(f32).reshape(bsz, seq, C_GROUPS, C_GROUP)
    lr = jnp.minimum(lam_re.astype(f32), -1e-4)
    li = lam_im.astype(f32)
    dt = jnp.exp(log_dt.astype(f32))[:, None]
    mag = jnp.exp(lr * dt)
    ar = mag * jnp.cos(li * dt)
    ai = mag * jnp.sin(li * dt)
    den = lr * lr + li * li
    nr = ar - 1.0
    cr = (nr * lr + ai * li) / den
    ci = (ai * lr - nr * li) / den
    br = b_re.astype(f32)
    bi = b_im.astype(f32)
    bbr = cr[..., None] * br - ci[..., None] * bi
    bbi = cr[..., None] * bi + ci[..., None] * br
    bu_r = jnp.einsum('gpc,bsgc->bsgp', bbr, uf)
    bu_i = jnp.einsum('gpc,bsgc->bsgp', bbi, uf)
    a_r = jnp.broadcast_to(ar, bu_r.shape)
    a_i = jnp.broadcast_to(ai, bu_i.shape)

    def combine(e1, e2):
        a1r, a1i, b1r, b1i = e1
        a2r, a2i, b2r, b2i = e2
        return (a2r * a1r - a2i * a1i, a2r * a1i + a2i * a1r,
                a2r * b1r - a2i * b1i + b2r, a2r * b1i + a2i * b1r + b2i)

    _, _, xr, xi = lax.associative_scan(combine, (a_r, a_i, bu_r, bu_i), axis=1)
    y = (jnp.einsum('gcp,bsgp->bsgc', c_re.astype(f32), xr)
         - jnp.einsum('gcp,bsgp->bsgc', c_im.astype(f32), xi))
    y = y + d_skip.astype(f32) * uf
    return y.reshape(bsz, seq, C_WIDTH).astype(u.dtype)


def even_mixer(h, w_in, ln_g, ln_b, w_s, b_s, w_gate2, b_gate, gla_norm_g, w_out):
    p = h @ w_in
    za, q, k, v, r, glr = jnp.split(p, EVEN_SPLITS, axis=-1)
    y_a = spatial_gating(jax.nn.gelu(za), ln_g, ln_b, w_s, b_s)
    gk = jax.nn.log_sigmoid((glr @ w_gate2 + b_gate).astype(jnp.float32)) / B_GATE_TAU
    y_b = gla(q, k, v, gk, r, gla_norm_g)
    return jnp.concatenate([y_a, y_b.astype(y_a.dtype)], axis=-1) @ w_out


def odd_mixer(h, w_in, lam_re, lam_im, log_dt, b_re, b_im, c_re, c_im, d_skip, w_glu, b_glu, w_out):
    y = s5_ssm(h @ w_in, lam_re, lam_im, log_dt, b_re, b_im, c_re, c_im, d_skip)
    z = jax.nn.gelu(y)
    z = z * jax.nn.sigmoid(z @ w_glu + b_glu)
    return z @ w_out


def setup_inputs(seed: int = 0) -> dict:
    key = jax.random.key(seed)
    ks = jax.random.split(key, 32)
    f32 = jnp.float32

    def nrm(k, shape, scale):
        return jax.random.normal(k, shape, f32) * scale

    n_idx = jnp.arange(C_STATE, dtype=f32)
    return {
        'x': nrm(ks[0], (BATCH, SEQ, D_MODEL), 1.0),
        'norm_g': 1.0 + nrm(ks[1], (DEPTH, 6, D_MODEL), 0.02),
        'ffn_w_gate': nrm(ks[2], (DEPTH, 2, D_MODEL, D_FF), D_MODEL ** -0.5),
        'ffn_w_up': nrm(ks[3], (DEPTH, 2, D_MODEL, D_FF), D_MODEL ** -0.5),
        'ffn_w_down': nrm(ks[4], (DEPTH, 2, D_FF, D_MODEL), D_FF ** -0.5),
        'ev_w_in': nrm(ks[5], (N_EVEN, D_MODEL, EVEN_IN), D_MODEL ** -0.5),
        'ev_ln_g': 1.0 + nrm(ks[6], (N_EVEN, A_WIDTH), 0.02),
        'ev_ln_b': nrm(ks[7], (N_EVEN, A_WIDTH), 0.02),
        'ev_w_s': nrm(ks[8], (N_EVEN, A_HEADS, A_CHUNK, A_CHUNK), A_CHUNK ** -0.5),
        'ev_b_s': 1.0 + nrm(ks[9], (N_EVEN, A_HEADS, A_CHUNK), 0.02),
        'ev_w_gate2': nrm(ks[10], (N_EVEN, B_GATE_RANK, B_KEY), B_GATE_RANK ** -0.5),
        'ev_b_gate': nrm(ks[11], (N_EVEN, B_KEY), 0.1),
        'ev_gla_norm_g': 1.0 + nrm(ks[12], (N_EVEN, B_DV), 0.02),
        'ev_w_out': nrm(ks[13], (N_EVEN, EVEN_MIX, D_MODEL), EVEN_MIX ** -0.5),
        'od_w_in': nrm(ks[14], (N_ODD, D_MODEL, C_WIDTH), D_MODEL ** -0.5),
        'od_lam_re': -0.5 + nrm(ks[15], (N_ODD, C_GROUPS, C_STATE), 0.01),
        'od_lam_im': math.pi * n_idx + nrm(ks[16], (N_ODD, C_GROUPS, C_STATE), 0.01),
        'od_log_dt': jax.random.uniform(ks[17], (N_ODD, C_GROUPS), f32, math.log(1e-3), math.log(1e-1)),
        'od_b_re': nrm(ks[18], (N_ODD, C_GROUPS, C_STATE, C_GROUP), (2 * C_GROUP) ** -0.5),
        'od_b_im': nrm(ks[19], (N_ODD, C_GROUPS, C_STATE, C_GROUP), (2 * C_GROUP) ** -0.5),
        'od_c_re': nrm(ks[20], (N_ODD, C_GROUPS, C_GROUP, C_STATE), C_STATE ** -0.5),
        'od_c_im': nrm(ks[21], (N_ODD, C_GROUPS, C_GROUP, C_STATE), C_STATE ** -0.5),
        'od_d': nrm(ks[22], (N_ODD, C_GROUPS, C_GROUP), 1.0),
        'od_w_glu': nrm(ks[23], (N_ODD, C_WIDTH, C_WIDTH), C_WIDTH ** -0.5),
        'od_b_glu': nrm(ks[24], (N_ODD, C_WIDTH), 0.02),
        'od_w_out': nrm(ks[25], (N_ODD, C_WIDTH, D_MODEL), C_WIDTH ** -0.5),
    }


def reference(x, norm_g, ffn_w_gate, ffn_w_up, ffn_w_down,
              ev_w_in, ev_ln_g, ev_ln_b, ev_w_s, ev_b_s, ev_w_gate2, ev_b_gate, ev_gla_norm_g, ev_w_out,
              od_w_in, od_lam_re, od_lam_im, od_log_dt, od_b_re, od_b_im, od_c_re, od_c_im, od_d,
              od_w_glu, od_b_glu, od_w_out):
    h = x
    for l in range(DEPTH):
        i = l // 2
        g = norm_g[l]
        f = swiglu(rmsnorm(h, g[0]), ffn_w_gate[l, 0], ffn_w_up[l, 0], ffn_w_down[l, 0])
        h = h + 0.5 * rmsnorm(f, g[1])
        m_in = rmsnorm(h, g[2])
        if l % 2 == 0:
            m = even_mixer(m_in, ev_w_in[i], ev_ln_g[i], ev_ln_b[i], ev_w_s[i], ev_b_s[i],
                           ev_w_gate2[i], ev_b_gate[i], ev_gla_norm_g[i], ev_w_out[i])
        else:
            m = odd_mixer(m_in, od_w_in[i], od_lam_re[i], od_lam_im[i], od_log_dt[i],
                          od_b_re[i], od_b_im[i], od_c_re[i], od_c_im[i], od_d[i],
                          od_w_glu[i], od_b_glu[i], od_w_out[i])
        h = h + rmsnorm(m, g[3])
        f = swiglu(rmsnorm(h, g[4]), ffn_w_gate[l, 1], ffn_w_up[l, 1], ffn_w_down[l, 1])
        h = h + 0.5 * rmsnorm(f, g[5])
    return h
```

```python
import os
import numpy as np
import concourse.bass as bass
import concourse.mybir as mybir
from concourse.bass_utils import run_bass_kernel_spmd

F32 = mybir.dt.float32
BF16 = mybir.dt.bfloat16
AF = mybir.ActivationFunctionType
ALU = mybir.AluOpType
AX = mybir.AxisListType

NCORES = 8
SEQ = 8192
TOK = SEQ // NCORES
NTT = TOK // 128
D = 2048
KC = D // 128
DFF = 5632
FC = DFF // 128
EPS = 1e-6

STAGES = int(os.environ.get("MK_STAGES", "6"))
PLAN = os.environ.get("MK_PLAN", "full")
ODDSTOP = int(os.environ.get("MK_ODDSTOP", "99"))
CCN = int(os.environ.get("MK_CCN", str(NCORES)))


class Tracker:
    def __init__(self, nc, psem, dsem):
        self.nc = nc
        self.E = {"pe": nc.tensor, "act": nc.scalar, "dve": nc.vector, "pool": nc.gpsimd, "sp": nc.sync}
        self.psem = psem
        self.dsem = dsem
        self.cnt = {e: 0 for e in self.E}
        self.dcnt = {n: 0 for n in dsem}
        self.waited = {}
        self.lw = {}
        self.rd = {}
        self.pend = {e: ([], []) for e in self.E}

    def _deps(self, reads, writes):
        deps = {}

        def add(ev):
            if ev is None:
                return
            if ev[0] not in deps or deps[ev[0]][2] < ev[2]:
                deps[ev[0]] = ev

        for k in reads:
            add(self.lw.get(k))
        for k in writes:
            add(self.lw.get(k))
            for ev in self.rd.get(k, {}).values():
                add(ev)
        return list(deps.values())

    def _wait(self, e, deps):
        for (sn, s, v) in deps:
            if self.waited.get((e, sn), 0) < v:
                self.E[e].wait_ge(s, v)
                self.waited[(e, sn)] = v

    def _record(self, ev, reads, writes):
        for k in reads:
            self.rd.setdefault(k, {})[ev[0]] = ev
        for k in writes:
            self.lw[k] = ev
            self.rd[k] = {}

    def _check_pending(self, e, reads, writes):
        for e2, (pr, pw) in self.pend.items():
            if e2 == e:
                continue
            for k in list(reads) + list(writes):
                assert k not in pw, ("pending write consumed", k, e, e2)
            for k in writes:
                assert k not in pr, ("pending read overwritten", k, e, e2)

    def op(self, e, fn, reads=(), writes=(), signal=True):
        self._check_pending(e, reads, writes)
        self._wait(e, self._deps(reads, writes))
        ins = fn(self.E[e])
        if signal:
            self.cnt[e] += 1
            ins.then_inc(self.psem[e], 1)
            ev = (e, self.psem[e], self.cnt[e])
            pr, pw = self.pend[e]
            self._record(ev, list(reads) + pr, list(writes) + pw)
            self.pend[e] = ([], [])
        else:
            self.pend[e][0].extend(reads)
            self.pend[e][1].extend(writes)
        return ins

    def dma(self, q, sem, pairs, reads=(), writes=()):
        self._check_pending(q, reads, writes)
        self._wait(q, self._deps(reads, writes))
        for (out, in_) in pairs:
            self.E[q].dma_start(out=out, in_=in_).then_inc(self.dsem[sem], 16)
            self.dcnt[sem] += 16
        ev = ("d:" + sem, self.dsem[sem], self.dcnt[sem])
        self._record(ev, list(reads), list(writes))
        return ev

    def barrier(self):
        for e, (pr, pw) in self.pend.items():
            assert not pr and not pw, ("pending at barrier", e)
        evs = [(e, self.psem[e], self.cnt[e]) for e in self.E if self.cnt[e] > 0]
        evs += [("d:" + n, self.dsem[n], c) for n, c in self.dcnt.items() if c > 0]
        for e in self.E:
            self._wait(e, evs)
        self.lw = {}
        self.rd = {}


def build_program():
    nc = bass.Bass("TRN2", target_bir_lowering=False)

    def din(name, shape):
        return nc.dram_tensor(name, list(shape), F32, kind="ExternalInput").ap()

    x_d = din("x", [TOK, D])
    normg_d = din("norm_g", [2, 6, D])
    gcol_d = din("gcol", [128, 12 * KC])
    if PLAN == "full":
        wg_d = din("ffn_w_gate", [2, 2, D, DFF])
        wu_d = din("ffn_w_up", [2, 2, D, DFF])
        wd_d = din("ffn_w_down", [2, 2, DFF, D])
    evwin_d = din("ev_w_in", [D, 5136])
    evwout_d = din("ev_w_out", [D, D])
    evlng_d = din("ev_ln_g", [1, 1024])
    evlnb_d = din("ev_ln_b", [1, 1024])
    evwsT_d = din("ev_w_sT", [128, 8 * 128])
    evbsT_d = din("ev_b_sT", [128, 8])
    evwg2_d = din("ev_w_gate2", [16, 512])
    evbg_d = din("ev_b_gate", [1, 512])
    evgng_d = din("ev_gla_norm_g", [1, 256])
    cmask_d = din("cmask", [128, 8])
    odwin_d = din("od_w_in", [D, 1024])
    odwglu_d = din("od_w_glu", [1024, 1024])
    odwout_d = din("od_w_out", [1024, D])
    odlre_d = din("od_lamreT", [128, 32])
    odlim_d = din("od_lamimT", [128, 32])
    odldt_d = din("od_ldtT", [128, 32])
    odbre_d = din("od_breT", [128, 512])
    odbim_d = din("od_bimT", [128, 512])
    odcre_d = din("od_creT", [128, 512])
    odcim_d = din("od_cimT", [128, 512])
    oddcol_d = din("od_dcol", [128, 8])
    odbglu_d = din("od_bglucol", [128, 8])
    out_d = nc.dram_tensor("out", [TOK, D], F32, kind="ExternalOutput").ap()
    cc_in = nc.dram_tensor("cc_in", [128, 1028], F32, kind="Internal").ap()
    cc_out = nc.dram_tensor("cc_out", [NCORES * 128, 1028], F32, kind="Internal").ap()
    cc2_in = nc.dram_tensor("cc2_in", [128, 64], F32, kind="Internal").ap()
    cc2_out = nc.dram_tensor("cc2_out", [NCORES * 128, 64], F32, kind="Internal").ap()

    import contextlib
    es = contextlib.ExitStack()
    with es:
        def sb(name, shape, dt):
            return es.enter_context(nc.sbuf_tensor(name, list(shape), dt))

        H = sb("H", [128, NTT, D], F32)
        XT = sb("XT", [128, KC, TOK], BF16)
        BIG = sb("BIG", [128, FC * TOK], BF16)
        WB = sb("WB", [128, 8192], BF16)
        TMP = sb("TMP", [128, 2, 512], F32)
        ident = sb("ident", [128, 128], F32)
        gcur = sb("gcur", [128, KC], F32)
        cols = sb("cols", [128, 320], F32)
        TRIP = sb("TRIP", [128, 128], F32)
        SUFP = sb("SUFP", [128, 128], F32)
        MASK = sb("MASK", [128, 128], BF16)
        identb = sb("identb", [128, 128], BF16)
        PS = [es.enter_context(nc.psum_tensor(f"ps{i}", [128, 512], F32)) for i in range(8)]

        eng_names = ["pe", "act", "dve", "pool", "sp"]
        psem = {e: es.enter_context(nc.semaphore("p_" + e)) for e in eng_names}
        dnames = ["x", "gc", "w0", "w1", "v0", "v1", "v2", "v3", "gb", "out", "c0", "c1", "c2", "ex"]
        dsem = {n: es.enter_context(nc.semaphore("d_" + n)) for n in dnames}
        T = Tracker(nc, psem, dsem)

        HID = BIG[:].rearrange("p (f t) -> p f t", f=FC)
        F16 = XT[:].rearrange("p k t -> p (k t)").rearrange("p (a n) -> p a n", a=NTT)
        BIGF = BIG[:].bitcast(F32)
        xs = BIGF[:, 0:2 * D].rearrange("p (s n) -> p s n", s=2)
        junk = BIG[:, 4 * D + 0:4 * D + D]
        GBv = BIGF[:, 16384:18432]
        tmpf = BIGF[:, 18432:22528].rearrange("p (s n) -> p s n", s=2)
        FM = BIGF[:, 0:16384].rearrange("p (a n) -> p a n", a=NTT)
        XTF = XT[:].rearrange("p k t -> p (k t)").bitcast(F32)
        SPR = XTF[:, 0:4096]
        MISC = XTF[:, 4096:8192]
        W1 = WB[:].rearrange("p (s m k n) -> p s m k n", s=2, m=2, k=KC)
        W2 = WB[:].rearrange("p (s k n) -> p s k n", s=4, k=4)
        ssq = cols[:, 0:8]
        std = cols[:, 8:16]
        rstd = cols[:, 16:24]
        ss2 = cols[:, 24:32]
        ssq2 = cols[:, 32:96]
        MV = cols[:, 96:112].rearrange("p (a b) -> p a b", b=2)
        sm = cols[:, 112:128]
        BNS = cols[:, 128:320].rearrange("p (a k s) -> p a k s", a=NTT, k=4)

        T.op("pool", lambda e: e.memset(ident[:], 1.0), writes=["ident"])
        T.op("pool", lambda e: e.affine_select(out=ident[:], in_=ident[:], pattern=[[-1, 128]],
                                               compare_op=ALU.is_equal, fill=0.0, base=0,
                                               channel_multiplier=1), reads=["ident"], writes=["ident"])
        T.op("pool", lambda e: e.memset(identb[:], 1.0), writes=["identb"])
        T.op("pool", lambda e: e.affine_select(out=identb[:], in_=identb[:], pattern=[[-1, 128]],
                                               compare_op=ALU.is_equal, fill=0.0, base=0,
                                               channel_multiplier=1), reads=["identb"], writes=["identb"])
        T.op("pool", lambda e: e.memset(TRIP[:], -1.0 / 16.0), writes=["TRIP"])
        T.op("pool", lambda e: e.affine_select(out=TRIP[:], in_=TRIP[:], pattern=[[1, 128]], compare_op=ALU.is_ge,
                                               fill=0.0, base=0, channel_multiplier=-1), writes=["TRIP"])
        T.op("pool", lambda e: e.memset(TRIP[0:64, 64:128], 0.0), writes=["TRIP"])
        T.op("pool", lambda e: e.memset(SUFP[:], -1.0 / 16.0), writes=["SUFP"])
        T.op("pool", lambda e: e.affine_select(out=SUFP[:], in_=SUFP[:], pattern=[[-1, 128]], compare_op=ALU.is_gt,
                                               fill=0.0, base=0, channel_multiplier=1), writes=["SUFP"])
        T.op("pool", lambda e: e.memset(SUFP[64:128, 0:64], 0.0), writes=["SUFP"])
        T.op("pool", lambda e: e.memset(MASK[:], 1.0), writes=["MASK"])
        T.op("pool", lambda e: e.affine_select(out=MASK[:], in_=MASK[:], pattern=[[1, 128]], compare_op=ALU.is_ge,
                                               fill=0.0, base=0, channel_multiplier=-1), writes=["MASK"])
        T.op("pool", lambda e: e.memset(MASK[0:64, 64:128], 0.0), writes=["MASK"])
        T.dma("sp", "x", [(H[:, tt, :], x_d[tt * 128:(tt + 1) * 128, :]) for tt in range(NTT)],
              writes=[("H", tt) for tt in range(NTT)])

        def prenorm(gi):
            T.dma("sp", "gc", [(gcur[:], gcol_d[:, gi * KC:(gi + 1) * KC])], writes=["gcol"])
            for tt in range(NTT):
                s = tt % 2
                T.op("act", lambda e: e.activation(out=junk, in_=H[:, tt, :], func=AF.Square,
                                                   accum_out=ssq[:, tt:tt + 1]),
                     reads=[("H", tt)], writes=["junk", ("ssq", tt)])
                T.op("act", lambda e: e.activation(out=std[:, tt:tt + 1], in_=ssq[:, tt:tt + 1], func=AF.Sqrt,
                                                   scale=1.0 / D, bias=EPS),
                     reads=[("ssq", tt)], writes=[("std", tt)])
                T.op("dve", lambda e: e.reciprocal(out=rstd[:, tt:tt + 1], in_=std[:, tt:tt + 1]),
                     reads=[("std", tt)], writes=[("rstd", tt)])
                T.op("act", lambda e: e.activation(out=xs[:, s, :], in_=H[:, tt, :], func=AF.Copy,
                                                   scale=rstd[:, tt:tt + 1]),
                     reads=[("H", tt), ("rstd", tt)], writes=[("xs", s)])
                for kq in range(4):
                    b = PS[kq % 2]
                    for j in range(4):
                        kc = kq * 4 + j
                        T.op("pe", lambda e: e.transpose(out=b[:, j * 128:(j + 1) * 128],
                                                         in_=xs[:, s, kc * 128:(kc + 1) * 128],
                                                         identity=ident[:]),
                             reads=[("xs", s), "ident"], writes=[("ps", kq % 2)], signal=(j == 3))
                    T.op("dve", lambda e: e.tensor_tensor(
                        out=XT[:, kq * 4:(kq + 1) * 4, tt * 128:(tt + 1) * 128],
                        in0=b[:].rearrange("p (a t) -> p a t", a=4),
                        in1=gcur[:, kq * 4:(kq + 1) * 4].unsqueeze(2).to_broadcast([128, 4, 128]),
                        op=ALU.mult),
                         reads=[("ps", kq % 2), "gcol"], writes=[("XT", tt), "XTall"])

        def postnorm_residual(l, j, wres, last, Fsrc=None, nss=4):
            if Fsrc is None:
                Fsrc = F16
            T.dma("sp", "gb", [(GBv, normg_d[l, j:j + 1, :].to_broadcast([128, D]))], writes=["GB"])
            for tt in range(NTT):
                s = tt % 2
                T.op("dve", lambda e: e.reduce_sum(out=ss2[:, tt:tt + 1], in_=ssq2[:, tt * 8:tt * 8 + nss], axis=AX.X),
                     reads=[("ssq2", tt)], writes=[("ss2", tt)])
                T.op("act", lambda e: e.activation(out=std[:, tt:tt + 1], in_=ss2[:, tt:tt + 1], func=AF.Sqrt,
                                                   scale=1.0 / (D * wres * wres), bias=EPS / (wres * wres)),
                     reads=[("ss2", tt)], writes=[("std", tt)])
                T.op("dve", lambda e: e.reciprocal(out=rstd[:, tt:tt + 1], in_=std[:, tt:tt + 1]),
                     reads=[("std", tt)], writes=[("rstd", tt)])
                T.op("dve", lambda e: e.tensor_tensor(out=tmpf[:, s, :], in0=Fsrc[:, tt, :], in1=GBv, op=ALU.mult),
                     reads=[("F16", tt), "GB"], writes=[("tmpf", s)])
                T.op("dve", lambda e: e.scalar_tensor_tensor(out=H[:, tt, :], in0=tmpf[:, s, :],
                                                             scalar=rstd[:, tt:tt + 1], op0=ALU.mult,
                                                             in1=H[:, tt, :], op1=ALU.add),
                     reads=[("tmpf", s), ("rstd", tt), ("H", tt)], writes=[("H", tt)])
                if last:
                    T.dma("sp", "out", [(out_d[tt * 128:(tt + 1) * 128, :], H[:, tt, :])], reads=[("H", tt)])

        def ffn(l, f):
            wgv = wg_d[l, f].rearrange("(k p) n -> p k n", p=128)
            wuv = wu_d[l, f].rearrange("(k p) n -> p k n", p=128)
            wdv = wd_d[l, f].rearrange("(k p) n -> p k n", p=128)
            prenorm(l * 6 + f * 4)
            T.barrier()
            i = 0
            for fc in range(FC):
                s = fc % 2
                T.dma("pool", f"w{s}", [(W1[:, s, 0], wgv[:, :, fc * 128:(fc + 1) * 128]),
                                        (W1[:, s, 1], wuv[:, :, fc * 128:(fc + 1) * 128])],
                      writes=[("w", s)])
                for half in range(2):
                    bg, bu = 2 * (i % 2), 2 * (i % 2) + 1
                    xk = [("XT", t) for t in range(4 * half, 4 * half + 4)] + ["XTall"]
                    for m, bank in ((0, bg), (1, bu)):
                        for kc in range(KC):
                            T.op("pe", lambda e: e.matmul(PS[bank][:], lhsT=W1[:, s, m, kc, :],
                                                          rhs=XT[:, kc, half * 512:(half + 1) * 512],
                                                          start=(kc == 0), stop=(kc == KC - 1)),
                                 reads=[("w", s)] + xk, writes=[("ps", bank)], signal=(kc == KC - 1))
                    T.op("act", lambda e: e.activation(out=TMP[:, i % 2, :], in_=PS[bg][:], func=AF.Silu),
                         reads=[("ps", bg)], writes=[("tmp", i % 2)])
                    T.op("dve", lambda e: e.tensor_tensor(out=HID[:, fc, half * 512:(half + 1) * 512],
                                                          in0=TMP[:, i % 2, :], in1=PS[bu][:], op=ALU.mult),
                         reads=[("tmp", i % 2), ("ps", bu)], writes=[("HID", fc, half)])
                    i += 1
            junk2 = TMP[:].rearrange("p a n -> p (a n)").bitcast(BF16)[:, 0:512]
            g = 0
            for oc in range(4):
                for kg in range(FC // 4):
                    s = g % 4
                    g += 1
                    T.dma("pool", f"v{s}", [(W2[:, s], wdv[:, kg * 4:(kg + 1) * 4, oc * 512:(oc + 1) * 512])],
                          writes=[("w2", s), ("w", s // 2)])
                    for kk in range(4):
                        k = kg * 4 + kk
                        for tt in range(NTT):
                            T.op("pe", lambda e: e.matmul(PS[tt][:], lhsT=HID[:, k, tt * 128:(tt + 1) * 128],
                                                          rhs=W2[:, s, kk, :], start=(k == 0), stop=(k == FC - 1)),
                                 reads=[("w2", s), ("HID", k, tt // 4)], writes=[("ps", tt)],
                                 signal=(k == FC - 1) or (kk == 3 and tt == NTT - 1))
                for tt in range(NTT):
                    T.op("act", lambda e: e.activation(out=junk2, in_=PS[tt][:], func=AF.Square,
                                                       accum_out=ssq2[:, tt * 8 + oc:tt * 8 + oc + 1]),
                         writes=[("ps", tt), "junk2", ("ssq2", tt)])
                    T.op("dve", lambda e: e.tensor_copy(out=F16[:, tt, oc * 512:(oc + 1) * 512], in_=PS[tt][:]),
                         writes=[("ps", tt), "XTall", ("F16", tt)])
            T.barrier()
            postnorm_residual(l, f * 4 + 1, 0.5, last=(stage_idx[0] == len(plan) - 1))
            T.barrier()


        bankctr = [0]

        def nextbank(n=4, base=0):
            b = base + bankctr[0] % n
            bankctr[0] += 1
            return b

        def even_mixer():
            import math
            win = evwin_d.rearrange("(k p) n -> p k n", p=128)
            wout = evwout_d.rearrange("(k p) n -> p k n", p=128)
            Wt = WB[:].rearrange("p (s k n) -> p s k n", s=2, k=KC)
            U = BIG[:, 0:8192].rearrange("p (a n) -> p a n", a=NTT)
            V = BIG[:, 8192:16384].rearrange("p (a n) -> p a n", a=NTT)
            GV = BIG[:, 16384:24576].rearrange("p (a n) -> p a n", a=NTT)
            R = BIG[:, 24576:32768].rearrange("p (a n) -> p a n", a=NTT)
            QT = BIG[:, 32768:36864].rearrange("p (h t) -> p h t", h=4)
            KT = BIG[:, 36864:40960].rearrange("p (h t) -> p h t", h=4)
            KR = BIG[:, 40960:45056].rearrange("p (a n) -> p a n", a=NTT)
            V32 = TMP[:, 0, :].rearrange("p (s n) -> p s n", s=2)
            GLRT = TMP[:, 1, :].bitcast(BF16)
            tile_ctr = [0]

            def load_tile(src_ap, ncols=256):
                sl = tile_ctr[0] % 2
                tile_ctr[0] += 1
                T.dma("pool", f"w{sl}", [(Wt[:, sl, :, 0:ncols], src_ap)], writes=[("w", sl)])
                return sl

            def a_unit(sl, tt, ncols=256):
                b = nextbank()
                for kc in range(KC):
                    T.op("pe", lambda e: e.matmul(PS[b][:, 0:ncols], lhsT=XT[:, kc, tt * 128:(tt + 1) * 128],
                                                  rhs=Wt[:, sl, kc, 0:ncols], start=(kc == 0), stop=(kc == KC - 1)),
                         reads=[("w", sl), "XTall"], writes=[("ps", b)], signal=(kc == KC - 1))
                return b

            def b_unit(sl, c0, m, half):
                b = nextbank()
                for kc in range(KC):
                    T.op("pe", lambda e: e.matmul(PS[b][0:m, :], lhsT=Wt[:, sl, kc, c0:c0 + m],
                                                  rhs=XT[:, kc, half * 512:(half + 1) * 512],
                                                  start=(kc == 0), stop=(kc == KC - 1)),
                         reads=[("w", sl), "XTall"], writes=[("ps", b)], signal=(kc == KC - 1))
                return b

            prenorm(2)
            T.barrier()
            sl = load_tile(win[:, :, 5120:5136], 16)
            for half in range(2):
                b = b_unit(sl, 0, 16, half)
                T.op("dve", lambda e: e.tensor_copy(out=GLRT[0:16, half * 512:(half + 1) * 512], in_=PS[b][0:16, :]),
                     writes=[("ps", b), "GLRT"])
            for c in (8, 9, 10, 11):
                sl = load_tile(win[:, :, c * 256:(c + 1) * 256])
                dst = QT if c < 10 else KT
                for hh in range(2):
                    h = (c % 2) * 2 + hh
                    for half in range(2):
                        b = b_unit(sl, hh * 128, 128, half)
                        T.op("act", lambda e: e.activation(out=dst[:, h, half * 512:(half + 1) * 512], in_=PS[b][:],
                                                           func=AF.Copy),
                             writes=[("ps", b), ("QK", c, hh, half)])
                if c >= 10:
                    for tt in range(NTT):
                        b = a_unit(sl, tt)
                        T.op("dve", lambda e: e.tensor_copy(out=KR[:, tt, (c - 10) * 256:(c - 9) * 256],
                                                            in_=PS[b][:, 0:256]),
                             writes=[("ps", b), ("KR", tt)])
            for c in range(0, 4):
                sl = load_tile(win[:, :, c * 256:(c + 1) * 256])
                for tt in range(NTT):
                    b = a_unit(sl, tt)
                    T.op("act", lambda e: e.activation(out=U[:, tt, c * 256:(c + 1) * 256], in_=PS[b][:, 0:256],
                                                       func=AF.Gelu),
                         writes=[("ps", b), ("U", tt)])
            i = 0
            for c in range(4, 8):
                sl = load_tile(win[:, :, c * 256:(c + 1) * 256])
                for tt in range(NTT):
                    b = a_unit(sl, tt)
                    T.op("act", lambda e: e.activation(out=V32[:, i % 2, :], in_=PS[b][:, 0:256], func=AF.Gelu),
                         writes=[("ps", b), ("V32", i % 2)])
                    T.op("dve", lambda e: e.bn_stats(out=BNS[:, tt, c - 4, :], in_=V32[:, i % 2, :]),
                         reads=[("V32", i % 2)], writes=[("BNS", tt)])
                    T.op("dve", lambda e: e.tensor_copy(out=V[:, tt, (c - 4) * 256:(c - 3) * 256], in_=V32[:, i % 2, :]),
                         reads=[("V32", i % 2)], writes=[("V", tt)])
                    i += 1
            for c in range(12, 16):
                sl = load_tile(win[:, :, c * 256:(c + 1) * 256])
                for tt in range(NTT):
                    b = a_unit(sl, tt)
                    T.op("dve", lambda e: e.tensor_copy(out=GV[:, tt, (c - 12) * 256:(c - 11) * 256], in_=PS[b][:, 0:256]),
                         writes=[("ps", b), ("GV", tt)])
            for c in range(16, 20):
                sl = load_tile(win[:, :, c * 256:(c + 1) * 256])
                for tt in range(NTT):
                    b = a_unit(sl, tt)
                    T.op("act", lambda e: e.activation(out=R[:, tt, (c - 16) * 256:(c - 15) * 256], in_=PS[b][:, 0:256],
                                                       func=AF.Silu),
                         writes=[("ps", b), ("R", tt)])
            T.barrier()
            SP = SPR.rearrange("p (a n) -> p a n", a=NTT)
            BGv = MISC[:, 0:512]
            WG2 = MISC[:, 512:768].bitcast(BF16)
            EB = MISC[:, 1024:2048].rearrange("p (s n) -> p s n", s=2)
            DEC = MISC[:, 4032:4096].rearrange("p (h c) -> p h c", h=4)
            T.dma("sp", "c0", [(BGv, evbg_d[0:1, :].to_broadcast([128, 512]))], writes=["BG"])
            T.dma("pool", "c1", [(WG2[0:16, :], evwg2_d[:, :])], writes=["WG2"])
            for tt in range(NTT):
                b = nextbank()
                x = tt % 2
                T.op("pe", lambda e: e.matmul(PS[b][:], lhsT=GLRT[0:16, tt * 128:(tt + 1) * 128], rhs=WG2[0:16, :],
                                              start=True, stop=True),
                     reads=["GLRT", "WG2"], writes=[("ps", b)])
                T.op("dve", lambda e: e.tensor_tensor(out=EB[:, x, :], in0=PS[b][:], in1=BGv, op=ALU.add),
                     reads=["BG"], writes=[("ps", b), ("EB", x)])
                T.op("act", lambda e: e.activation(out=EB[:, x, :], in_=EB[:, x, :], func=AF.Exp, scale=-1.0),
                     writes=[("EB", x)])
                T.op("act", lambda e: e.activation(out=SP[:, tt, :], in_=EB[:, x, :], func=AF.Ln, bias=1.0),
                     reads=[("EB", x)], writes=[("SP", tt)])
            lnscale = math.log(128.0 ** -0.5)
            for half in range(2):
                for h in range(4):
                    b = nextbank()
                    for j in range(4):
                        tt = half * 4 + j
                        T.op("pe", lambda e: e.matmul(PS[b][:, j * 128:(j + 1) * 128], lhsT=SP[:, tt, h * 128:(h + 1) * 128],
                                                      rhs=TRIP[:], start=True, stop=True),
                             reads=[("SP", tt), "TRIP"], writes=[("ps", b)], signal=(j == 3))
                    T.op("act", lambda e: e.activation(out=EB[:, 0, :], in_=PS[b][:], func=AF.Exp, bias=lnscale),
                         writes=[("ps", b), ("EB", 0)])
                    T.op("act", lambda e: e.activation(out=EB[:, 1, :], in_=PS[b][:], func=AF.Exp, scale=-1.0),
                         writes=[("ps", b), ("EB", 1)])
                    T.op("act", lambda e: e.activation(out=DEC[:, h, half * 8:(half + 1) * 8], in_=PS[b][:, 63:512:64],
                                                       func=AF.Exp),
                         writes=[("ps", b), "DEC"])
                    T.op("dve", lambda e: e.tensor_tensor(out=QT[:, h, half * 512:(half + 1) * 512],
                                                          in0=QT[:, h, half * 512:(half + 1) * 512], in1=EB[:, 0, :],
                                                          op=ALU.mult),
                         reads=[("EB", 0)], writes=[("QT", h, half)])
                    T.op("dve", lambda e: e.tensor_tensor(out=KT[:, h, half * 512:(half + 1) * 512],
                                                          in0=KT[:, h, half * 512:(half + 1) * 512], in1=EB[:, 1, :],
                                                          op=ALU.mult),
                         reads=[("EB", 1)], writes=[("KT", h, half)])
            for tt in range(NTT):
                b = nextbank()
                x = tt % 2
                T.op("pe", lambda e: e.matmul(PS[b][:], lhsT=SUFP[:], rhs=SP[:, tt, :], start=True, stop=True),
                     reads=[("SP", tt), "SUFP"], writes=[("ps", b)])
                T.op("act", lambda e: e.activation(out=EB[:, x, :], in_=PS[b][:], func=AF.Exp),
                     writes=[("ps", b), ("EB", x)])
                T.op("dve", lambda e: e.tensor_tensor(out=KR[:, tt, :], in0=KR[:, tt, :], in1=EB[:, x, :], op=ALU.mult),
                     reads=[("EB", x)], writes=[("KR", tt)])
            T.barrier()
            LNG = SPR[:, 0:1024]
            LNB = SPR[:, 1024:2048]
            WS32 = SPR[:, 2048:3072].rearrange("p (h t) -> p h t", h=8)
            VT = SPR[:, 3072:4096]
            WS = MISC[:, 3000:3512].bitcast(BF16).rearrange("p (h t) -> p h t", h=8)
            BST = MISC[:, 3512:3520]
            T.dma("sp", "c0", [(LNG, evlng_d[0:1, :].to_broadcast([128, 1024])),
                               (LNB, evlnb_d[0:1, :].to_broadcast([128, 1024])),
                               (WS32.rearrange("p h t -> p (h t)"), evwsT_d[:, :]),
                               (BST, evbsT_d[:, :])], writes=["LNG", "LNB", "WS32", "BST"])
            T.op("pool", lambda e: e.affine_select(out=WS[:], in_=WS32, pattern=[[0, 8], [1, 128]], compare_op=ALU.is_ge,
                                                   fill=0.0, base=0, channel_multiplier=-1),
                 reads=["WS32"], writes=["WS"])
            for tt in range(NTT):
                T.op("dve", lambda e: e.bn_aggr(out=MV[:, tt, :], in_=BNS[:, tt, :, :].rearrange("p k s -> p (k s)")),
                     reads=[("BNS", tt)], writes=[("MV", tt)])
                T.op("act", lambda e: e.activation(out=std[:, tt:tt + 1], in_=MV[:, tt, 1:2], func=AF.Sqrt, bias=EPS),
                     reads=[("MV", tt)], writes=[("std", tt)])
                T.op("dve", lambda e: e.reciprocal(out=rstd[:, tt:tt + 1], in_=std[:, tt:tt + 1]),
                     reads=[("std", tt)], writes=[("rstd", tt)])
                T.op("dve", lambda e: e.tensor_scalar(out=VT, in0=V[:, tt, :], scalar1=MV[:, tt, 0:1],
                                                      scalar2=rstd[:, tt:tt + 1], op0=ALU.subtract, op1=ALU.mult),
                     reads=[("MV", tt), ("rstd", tt)], writes=["VT"])
                T.op("dve", lambda e: e.tensor_tensor(out=VT, in0=VT, in1=LNG, op=ALU.mult), reads=["LNG"], writes=["VT"])
                T.op("dve", lambda e: e.tensor_tensor(out=V[:, tt, :], in0=VT, in1=LNB, op=ALU.add),
                     reads=["VT", "LNB"], writes=[("V", tt)])
                bb = 4 + 2 * (tt % 2)
                for h in range(8):
                    T.op("pe", lambda e: e.matmul(PS[bb + h // 4][:, (h % 4) * 128:(h % 4 + 1) * 128], lhsT=WS[:, h, :],
                                                  rhs=V[:, tt, h * 128:(h + 1) * 128], start=True, stop=True),
                         reads=["WS", ("V", tt)], writes=[("ps", bb + h // 4)], signal=(h % 4 == 3))
                for h in range(8):
                    T.op("dve", lambda e: e.scalar_tensor_tensor(out=U[:, tt, h * 128:(h + 1) * 128],
                                                                 in0=PS[bb + h // 4][:, (h % 4) * 128:(h % 4 + 1) * 128],
                                                                 scalar=BST[:, h:h + 1], op0=ALU.add,
                                                                 in1=U[:, tt, h * 128:(h + 1) * 128], op1=ALU.mult),
                         reads=["BST"], writes=[("ps", bb + h // 4), ("U", tt)])
            T.barrier()
            GNG = SPR[:, 0:256]
            S = SPR[:, 256:1280].rearrange("p (h v) -> p h v", h=4)
            SB = SPR[:, 1280:2304].bitcast(BF16).rearrange("p (h s v) -> p h s v", h=4, s=2)
            OT = SPR[:, 2304:2816].rearrange("p (s v) -> p s v", s=2)
            STb = SPR[:, 2816:2944].bitcast(BF16).rearrange("p (s i) -> p s i", s=2)
            EX = MISC[:, 0:1028]
            tmpE = MISC[:, 1028:2052]
            DP = MISC[:, 2052:2056]
            CM = MISC[:, 2056:2064]
            DT = MISC[:, 2064:2068]
            T.dma("sp", "c0", [(GNG, evgng_d[0:1, :].to_broadcast([128, 256])), (CM, cmask_d[:, :])],
                  writes=["GNG", "CM"])
            T.op("dve", lambda e: e.memset(S.rearrange("p h v -> p (h v)"), 0.0), writes=["S"])

            def ds_mm(c, h, b, col0):
                tt, jj = c // 2, c % 2
                T.op("pe", lambda e: e.matmul(PS[b][:, col0:col0 + 256],
                                              lhsT=KR[64 * jj:64 * jj + 64, tt, h * 128:(h + 1) * 128],
                                              rhs=GV[64 * jj:64 * jj + 64, tt, h * 256:(h + 1) * 256],
                                              start=True, stop=True),
                     writes=[("ps", b)])

            def s_update(c, h, b, col0):
                T.op("dve", lambda e: e.scalar_tensor_tensor(out=S[:, h, :], in0=S[:, h, :], scalar=DEC[:, h, c:c + 1],
                                                             op0=ALU.mult, in1=PS[b][:, col0:col0 + 256], op1=ALU.add),
                     reads=["DEC"], writes=[("ps", b), "S"])

            for c in range(2 * NTT):
                for h in range(4):
                    b = nextbank()
                    ds_mm(c, h, b, 0)
                    s_update(c, h, b, 0)
            T.op("dve", lambda e: e.tensor_reduce(out=DT, in_=DEC, axis=AX.X, op=ALU.mult), reads=["DEC"], writes=["DT"])
            T.dma("sp", "c2", [(cc_in[:, 0:1024], S.rearrange("p h v -> p (h v)")), (cc_in[:, 1024:1028], DT)],
                  reads=["S", "DT"], writes=["cc_in"])
            T.op("pool", lambda e: e.collective_compute("AllGather", ALU.bypass, replica_groups=[list(range(CCN))],
                                                        ins=[cc_in[:, :]], outs=[cc_out[0:CCN * 128, :]]),
                 reads=["cc_in"], writes=["cc_out"])
            T.op("dve", lambda e: e.memset(S.rearrange("p h v -> p (h v)"), 0.0), writes=["S"])
            for j in range(CCN - 1):
                T.dma("sp", "ex", [(EX, cc_out[j * 128:(j + 1) * 128, :])], reads=["cc_out"], writes=["EX"])
                T.op("dve", lambda e: e.tensor_scalar(out=DP, in0=EX[:, 1024:1028], scalar1=-1.0, scalar2=CM[:, j:j + 1],
                                                      op0=ALU.add, op1=ALU.mult), reads=["EX", "CM"], writes=["DP"])
                T.op("dve", lambda e: e.tensor_scalar(out=DP, in0=DP, scalar1=1.0, scalar2=None, op0=ALU.add),
                     writes=["DP"])
                T.op("dve", lambda e: e.tensor_scalar(out=tmpE, in0=EX[:, 0:1024], scalar1=CM[:, j:j + 1], scalar2=None,
                                                      op0=ALU.mult), reads=["EX", "CM"], writes=["tmpE"])
                for h in range(4):
                    T.op("dve", lambda e: e.scalar_tensor_tensor(out=S[:, h, :], in0=S[:, h, :], scalar=DP[:, h:h + 1],
                                                                 op0=ALU.mult, in1=tmpE[:, h * 256:(h + 1) * 256],
                                                                 op1=ALU.add),
                         reads=["DP", "tmpE"], writes=["S"])
            for h in range(4):
                T.op("dve", lambda e: e.tensor_copy(out=SB[:, h, 0, :], in_=S[:, h, :]), reads=["S"], writes=[("SB", h, 0)])
            k = 0
            for tt in range(NTT):
                for h in range(4):
                    x = k % 2
                    k += 1
                    tk = slice(tt * 128, (tt + 1) * 128)
                    b1 = nextbank()
                    T.op("pe", lambda e: e.matmul(PS[b1][:, 0:128], lhsT=KT[:, h, tk], rhs=QT[:, h, tk],
                                                  start=True, stop=True), writes=[("ps", b1)])
                    T.op("dve", lambda e: e.tensor_tensor(out=STb[:, x, :], in0=PS[b1][:, 0:128], in1=MASK[:], op=ALU.mult),
                         reads=["MASK"], writes=[("ps", b1), ("STb", x)])
                    b2 = nextbank()
                    ds_mm(2 * tt, h, b2, 0)
                    ds_mm(2 * tt + 1, h, b2, 256)
                    s_update(2 * tt, h, b2, 0)
                    T.op("dve", lambda e: e.tensor_copy(out=SB[:, h, 1, :], in_=S[:, h, :]), reads=["S"], writes=[("SB", h, 1)])
                    b3 = nextbank()
                    T.op("pe", lambda e: e.matmul(PS[b3][:, 0:256], lhsT=STb[:, x, :], rhs=GV[:, tt, h * 256:(h + 1) * 256],
                                                  start=True, stop=False),
                         reads=[("STb", x)], writes=[("ps", b3)], signal=False)
                    T.op("pe", lambda e: e.matmul(PS[b3][0:64, 0:256], lhsT=QT[:, h, tt * 128:tt * 128 + 64],
                                                  rhs=SB[:, h, 0, :], start=False, stop=True),
                         reads=[("SB", h, 0)], writes=[("ps", b3)], signal=False)
                    T.op("pe", lambda e: e.matmul(PS[b3][64:128, 0:256], lhsT=QT[:, h, tt * 128 + 64:tt * 128 + 128],
                                                  rhs=SB[:, h, 1, :], start=False, stop=True),
                         reads=[("SB", h, 1)], writes=[("ps", b3)])
                    s_update(2 * tt + 1, h, b2, 256)
                    T.op("dve", lambda e: e.tensor_copy(out=SB[:, h, 0, :], in_=S[:, h, :]), reads=["S"], writes=[("SB", h, 0)])
                    T.op("act", lambda e: e.activation(out=OT[:, x, :], in_=PS[b3][:, 0:256], func=AF.Square,
                                                       accum_out=sm[:, x:x + 1]),
                         writes=[("ps", b3), ("OT", x), ("sm", x)])
                    T.op("act", lambda e: e.activation(out=sm[:, 2 + x:3 + x], in_=sm[:, x:x + 1], func=AF.Sqrt,
                                                       scale=1.0 / 256.0, bias=EPS),
                         reads=[("sm", x)], writes=[("sm2", x)])
                    T.op("dve", lambda e: e.reciprocal(out=sm[:, 4 + x:5 + x], in_=sm[:, 2 + x:3 + x]),
                         reads=[("sm2", x)], writes=[("sm4", x)])
                    T.op("dve", lambda e: e.tensor_tensor(out=OT[:, x, :], in0=PS[b3][:, 0:256], in1=GNG, op=ALU.mult),
                         reads=["GNG"], writes=[("ps", b3), ("OT", x)])
                    T.op("dve", lambda e: e.scalar_tensor_tensor(out=R[:, tt, h * 256:(h + 1) * 256], in0=OT[:, x, :],
                                                                 scalar=sm[:, 4 + x:5 + x], op0=ALU.mult,
                                                                 in1=R[:, tt, h * 256:(h + 1) * 256], op1=ALU.mult),
                         reads=[("OT", x), ("sm4", x)], writes=[("R", tt)])
            T.barrier()
            for tt in range(NTT):
                for src, base in ((U, 0), (R, 8)):
                    b = nextbank()
                    PSb = PS[b][:].bitcast(BF16)
                    for fb in range(8):
                        T.op("pe", lambda e: e.transpose(out=PSb[:, fb * 128:(fb + 1) * 128],
                                                         in_=src[:, tt, fb * 128:(fb + 1) * 128], identity=identb[:]),
                             reads=["identb"], writes=[("ps", b)], signal=(fb == 7))
                    eng = "dve" if base == 0 else "act"
                    if eng == "dve":
                        T.op("dve", lambda e: e.tensor_copy(out=XT[:, base:base + 8, tt * 128:(tt + 1) * 128],
                                                            in_=PSb.rearrange("p (a t) -> p a t", a=8)),
                             writes=[("ps", b), "XTall"])
                    else:
                        T.op("act", lambda e: e.activation(out=XT[:, base:base + 8, tt * 128:(tt + 1) * 128],
                                                           in_=PSb.rearrange("p (a t) -> p a t", a=8), func=AF.Copy),
                             writes=[("ps", b), "XTall"])
            T.barrier()
            junk3 = TMP[:].rearrange("p a n -> p (a n)").bitcast(BF16)[:, 0:256]
            for c in range(8):
                sl = load_tile(wout[:, :, c * 256:(c + 1) * 256])
                for tt in range(NTT):
                    b = a_unit(sl, tt)
                    T.op("act", lambda e: e.activation(out=junk3, in_=PS[b][:, 0:256], func=AF.Square,
                                                       accum_out=ssq2[:, tt * 8 + c:tt * 8 + c + 1]),
                         writes=[("ps", b), "junk3", ("ssq2", tt)])
                    T.op("dve", lambda e: e.tensor_copy(out=FM[:, tt, c * 256:(c + 1) * 256], in_=PS[b][:, 0:256]),
                         writes=[("ps", b), ("F16", tt)])
            T.barrier()
            postnorm_residual(0, 3, 1.0, last=(stage_idx[0] == len(plan) - 1), Fsrc=FM, nss=8)
            T.barrier()

        class _Stop(Exception):
            pass

        def chk(k):
            if k == ODDSTOP:
                raise _Stop()

        def odd_mixer():
            import math
            win = odwin_d.rearrange("(k p) n -> p k n", p=128)
            wglu = odwglu_d.rearrange("(k p) n -> p k n", p=128)
            wout = odwout_d.rearrange("(k p) n -> p k n", p=128)
            Wt = WB[:].rearrange("p (s k n) -> p s k n", s=2, k=KC)
            UT = BIG[:, 0:8192].rearrange("p (q t) -> p q t", q=8)
            YLf = BIGF[:, 4096:12288]
            YL = YLf.rearrange("p (q t) -> p q t", q=8)
            COS = BIGF[:, 12288:16384].rearrange("p (g j) -> p g j", g=32)
            SIN = BIGF[:, 16384:20480].rearrange("p (g j) -> p g j", g=32)
            Wv = BIGF[:, 20480:21504].rearrange("p (s n) -> p s n", s=2)
            Pv = BIGF[:, 21504:22528].bitcast(BF16).rearrange("p (s n) -> p s n", s=4)
            TMPD = BIGF[:, 20480:22528].rearrange("p (g j) -> p g j", g=32)
            G = XTF[:, 0:4096].bitcast(BF16).rearrange("p (s g j) -> p s g j", s=2, g=32)
            BBTp = XTF[:, 0:4096].bitcast(BF16).rearrange("p (g s m) -> p g s m", g=32, s=2)
            Zv = TMP[:]
            CT = XTF[:, 5120:6144].bitcast(BF16).rearrange("p (g s c) -> p g s c", g=32, s=2)
            CTp = WB[:].rearrange("p (g s c) -> p g s c", g=32, s=2)
            RM = cols[:, 312:316]
            Rb = XTF[:, 4096:4608].rearrange("p (g j) -> p g j", g=4)
            PAR = XTF[:, 7168:8192].rearrange("p (i g) -> p i g", i=32)
            P2 = cols[:, 128:320].rearrange("p (i g) -> p i g", i=6)
            GX = TMP[:].rearrange("p a n -> p (a n)")[:, 0:512].rearrange("p (r n) -> p r n", r=8)
            DCOL = cols[:, 288:296]
            BGL = cols[:, 296:304]
            INv = cols[:, 128:136].rearrange("p (s i) -> p s i", s=2)
            XE = cols[:, 192:256].rearrange("p (s g) -> p s g", s=2)
            ZL = cols[:, 224:288].rearrange("p (s g) -> p s g", s=2) if False else PAR[:, 30:32, :]
            RAW = YLf
            breT = RAW[:, 0:512].rearrange("p (g c) -> p g c", g=32)
            bimT = RAW[:, 512:1024].rearrange("p (g c) -> p g c", g=32)
            creT = RAW[:, 1024:1536].rearrange("p (g c) -> p g c", g=32)
            cimT = RAW[:, 1536:2048].rearrange("p (g c) -> p g c", g=32)
            BBr = RAW[:, 2048:2560].rearrange("p (g c) -> p g c", g=32)
            BBi = RAW[:, 2560:3072].rearrange("p (g c) -> p g c", g=32)
            BBX = RAW[:, 3072:5120].rearrange("p (s g c) -> p s g c", s=2, g=32)
            T16 = RAW[:, 5120:5632].rearrange("p (g c) -> p g c", g=32)
            GT = BIGF[:, 12288:20480].rearrange("p (s g j) -> p s g j", s=2, g=32)

            def P(i):
                return PAR[:, i, :]

            def D(fn, r=(), w=("PAR",)):
                T.op("dve", fn, reads=list(r), writes=list(w))

            def A(fn, r=(), w=("PAR",)):
                T.op("act", fn, reads=list(r), writes=list(w))

            def tt_(o, a, b, op, r=(), w=("PAR",)):
                D(lambda e: e.tensor_tensor(out=o, in0=a, in1=b, op=op), r, w)

            def bc(t2, n, g=32):
                return t2.unsqueeze(2).to_broadcast([128, g, n])

            def csq(ic, isn):
                tt_(P(13), P(ic), P(ic), ALU.mult)
                tt_(P(14), P(isn), P(isn), ALU.mult)
                tt_(P(15), P(ic), P(isn), ALU.mult)
                tt_(P(ic), P(13), P(14), ALU.subtract)
                D(lambda e: e.tensor_scalar(out=P(isn), in0=P(15), scalar1=2.0, scalar2=None, op0=ALU.mult))

            def cmul_par(ore, oim, are, aim, bre, bim):
                tt_(P(13), are, bre, ALU.mult)
                tt_(P(14), aim, bim, ALU.mult)
                tt_(P(15), are, bim, ALU.mult)
                tt_(ore, P(13), P(14), ALU.subtract)
                tt_(P(13), aim, bre, ALU.mult)
                tt_(oim, P(15), P(13), ALU.add)

            prenorm(8)
            T.barrier()
            tile_ctr = [0]

            def load_tile(src_ap, nk=KC, ncols=256):
                sl = tile_ctr[0] % 2
                tile_ctr[0] += 1
                T.dma("pool", f"w{sl}", [(Wt[:, sl, 0:nk, 0:ncols], src_ap)], writes=[("w", sl)])
                return sl

            for c in range(4):
                sl = load_tile(win[:, :, c * 256:(c + 1) * 256])
                for hh in range(2):
                    q = 2 * c + hh
                    for half in range(2):
                        b = nextbank()
                        for kc in range(KC):
                            T.op("pe", lambda e: e.matmul(PS[b][:], lhsT=Wt[:, sl, kc, hh * 128:(hh + 1) * 128],
                                                          rhs=XT[:, kc, half * 512:(half + 1) * 512],
                                                          start=(kc == 0), stop=(kc == KC - 1)),
                                 reads=[("w", sl), "XTall"], writes=[("ps", b)], signal=(kc == KC - 1))
                        T.op("act", lambda e: e.activation(out=UT[:, q, half * 512:(half + 1) * 512], in_=PS[b][:],
                                                           func=AF.Copy), writes=[("ps", b), ("UT", q)])
            T.barrier()
            chk(1)
            T.dma("sp", "c0", [(P(0), odlre_d[:, :]), (P(1), odlim_d[:, :]), (P(2), odldt_d[:, :]),
                               (breT.rearrange("p g c -> p (g c)"), odbre_d[:, :]),
                               (bimT.rearrange("p g c -> p (g c)"), odbim_d[:, :]),
                               (creT.rearrange("p g c -> p (g c)"), odcre_d[:, :]),
                               (cimT.rearrange("p g c -> p (g c)"), odcim_d[:, :]),
                               (DCOL, oddcol_d[:, :]), (BGL, odbglu_d[:, :])], writes=["PAR"])
            D(lambda e: e.tensor_scalar(out=P(0), in0=P(0), scalar1=-1e-4, scalar2=None, op0=ALU.min))
            A(lambda e: e.activation(out=P(2), in_=P(2), func=AF.Exp))
            tt_(P(13), P(0), P(2), ALU.mult)
            A(lambda e: e.activation(out=P(3), in_=P(13), func=AF.Exp))
            tt_(P(4), P(1), P(2), ALU.mult)
            A(lambda e: e.activation(out=P(6), in_=P(4), func=AF.Sin, scale=1.0 / 16.0))
            A(lambda e: e.activation(out=P(5), in_=P(4), func=AF.Sin, scale=-1.0 / 16.0, bias=math.pi / 2))
            for _ in range(4):
                csq(5, 6)
            tt_(P(7), P(3), P(5), ALU.mult)
            tt_(P(8), P(3), P(6), ALU.mult)
            tt_(P(13), P(0), P(0), ALU.mult)
            tt_(P(14), P(1), P(1), ALU.mult)
            tt_(P(9), P(13), P(14), ALU.add)
            D(lambda e: e.reciprocal(out=P(9), in_=P(9)))
            D(lambda e: e.tensor_scalar(out=P(10), in0=P(7), scalar1=-1.0, scalar2=None, op0=ALU.add))
            tt_(P(13), P(10), P(0), ALU.mult)
            tt_(P(14), P(8), P(1), ALU.mult)
            tt_(P(13), P(13), P(14), ALU.add)
            tt_(P(11), P(13), P(9), ALU.mult)
            tt_(P(13), P(8), P(0), ALU.mult)
            tt_(P(14), P(10), P(1), ALU.mult)
            tt_(P(13), P(13), P(14), ALU.subtract)
            tt_(P(12), P(13), P(9), ALU.mult)
            tt_(BBr, breT, bc(P(11), 16), ALU.mult)
            tt_(T16, bimT, bc(P(12), 16), ALU.mult)
            tt_(BBr, BBr, T16, ALU.subtract)
            tt_(BBi, bimT, bc(P(11), 16), ALU.mult)
            tt_(T16, breT, bc(P(12), 16), ALU.mult)
            tt_(BBi, BBi, T16, ALU.add)
            D(lambda e: e.memset(BBX.rearrange("p s g c -> p (s g c)"), 0.0))
            T.op("pool", lambda e: e.memset(RM, 1.0), writes=["RM"])
            T.op("pool", lambda e: e.affine_select(out=RM, in_=RM, pattern=[[-32, 4]], compare_op=ALU.is_ge, fill=0.0,
                                                   base=0, channel_multiplier=1), writes=["RM"])
            T.op("pool", lambda e: e.affine_select(out=RM, in_=RM, pattern=[[32, 4]], compare_op=ALU.is_ge, fill=0.0,
                                                   base=31, channel_multiplier=-1), writes=["RM"])
            for sidx, src in ((0, BBr), (1, BBi)):
                D(lambda e: e.tensor_copy(out=BBX[0:64, sidx, :, 0:16], in_=src[0:64, :, :]))
                D(lambda e: e.tensor_copy(out=BBX[64:128, sidx, :, 16:32], in_=src[64:128, :, :]))
            for sidx in range(2):
                for q in range(8):
                    b = nextbank()
                    T.op("pe", lambda e: e.transpose(out=PS[b][:, 0:128],
                                                     in_=BBX[:, sidx, 4 * q:4 * q + 4, :].rearrange("p g c -> p (g c)"),
                                                     identity=ident[:]),
                         reads=["PAR", "ident"], writes=[("ps", b)])
                    for i in range(4):
                        D(lambda e: e.tensor_scalar(out=BBTp[:, 4 * q + i, sidx, :], in0=PS[b][:, 0:128],
                                                    scalar1=RM[:, i:i + 1], scalar2=None, op0=ALU.mult),
                          r=["RM"], w=[("ps", b), "BBT"])
            D(lambda e: e.memset(CT.rearrange("p g s c -> p (g s c)"), 0.0), w=["CT"])
            D(lambda e: e.tensor_copy(out=CT[0:64, :, 0, 0:16], in_=creT[0:64, :, :]), r=["PAR"], w=["CT"])
            D(lambda e: e.tensor_copy(out=CT[64:128, :, 0, 16:32], in_=creT[64:128, :, :]), r=["PAR"], w=["CT"])
            D(lambda e: e.tensor_scalar(out=CT[0:64, :, 1, 0:16], in0=cimT[0:64, :, :], scalar1=-1.0, scalar2=None,
                                        op0=ALU.mult), r=["PAR"], w=["CT"])
            D(lambda e: e.tensor_scalar(out=CT[64:128, :, 1, 16:32], in0=cimT[64:128, :, :], scalar1=-1.0, scalar2=None,
                                        op0=ALU.mult), r=["PAR"], w=["CT"])
            D(lambda e: e.memset(CTp.rearrange("p g s c -> p (g s c)"), 0.0), w=["CTp"])
            for i in range(4):
                D(lambda e: e.tensor_copy(out=CTp[:, i:32:4, :, 32 * i:32 * i + 32], in_=CT[:, i:32:4, :, :]),
                  r=["CT"], w=["CTp"])

            def build_table(TC, TS, fc, fs):
                for k in range(7):
                    n = 1 << k
                    tmp = TMPD[:, :, 0:n]
                    tt_(TC[:, :, n:2 * n], TC[:, :, 0:n], bc(P(fc), n), ALU.mult)
                    tt_(tmp, TS[:, :, 0:n], bc(P(fs), n), ALU.mult)
                    tt_(TC[:, :, n:2 * n], TC[:, :, n:2 * n], tmp, ALU.subtract)
                    tt_(TS[:, :, n:2 * n], TS[:, :, 0:n], bc(P(fc), n), ALU.mult)
                    tt_(tmp, TC[:, :, 0:n], bc(P(fs), n), ALU.mult)
                    tt_(TS[:, :, n:2 * n], TS[:, :, n:2 * n], tmp, ALU.add)
                    if fc == 16:
                        csq(fc, fs)
                    else:
                        cmul_par(P(28), P(29), P(fc), P(fs), P(fc), P(fs))
                        D(lambda e: e.tensor_copy(out=P(fc), in_=P(28)))
                        D(lambda e: e.tensor_copy(out=P(fs), in_=P(29)))

            D(lambda e: e.memset(COS[:, :, 0:1], 1.0))
            D(lambda e: e.memset(SIN[:, :, 0:1], 0.0))
            D(lambda e: e.tensor_copy(out=P(16), in_=P(5)))
            D(lambda e: e.tensor_copy(out=P(17), in_=P(6)))
            build_table(COS, SIN, 16, 17)
            D(lambda e: e.tensor_copy(out=P(20), in_=P(16)))
            D(lambda e: e.tensor_copy(out=P(21), in_=P(17)))
            T.barrier()
            chk(2)
            for a in range(NTT):
                for q in range(8):
                    gsl = slice(4 * q, 4 * q + 4)
                    cosb = COS[:, gsl, :].rearrange("p g j -> p (g j)")
                    sinb = SIN[:, gsl, :].rearrange("p g j -> p (g j)")
                    br_, bi_ = nextbank(2, 0), nextbank(2, 0)
                    for sidx, bk in ((0, br_), (1, bi_)):
                        for i in range(4):
                            T.op("pe", lambda e: e.matmul(PS[bk][:, i * 128:(i + 1) * 128],
                                                          lhsT=BBTp[:, 4 * q + i, sidx, :],
                                                          rhs=UT[:, q, a * 128:(a + 1) * 128],
                                                          start=True, stop=True),
                                 writes=[("ps", bk)], signal=(i == 3))
                    tt_(Wv[:, 0, :], PS[br_][:], cosb, ALU.mult, w=[("ps", br_), "W0"])
                    tt_(Zv[:, 0, :], PS[bi_][:], sinb, ALU.mult, w=[("ps", bi_), "Z0"])
                    tt_(Wv[:, 0, :], Wv[:, 0, :], Zv[:, 0, :], ALU.add, w=["W0", "Z0"])
                    tt_(Wv[:, 1, :], PS[bi_][:], cosb, ALU.mult, w=[("ps", bi_), "W1"])
                    tt_(Zv[:, 1, :], PS[br_][:], sinb, ALU.mult, w=[("ps", br_), "Z1"])
                    tt_(Wv[:, 1, :], Wv[:, 1, :], Zv[:, 1, :], ALU.subtract, w=["W1", "Z1"])
                    if a > 0:
                        tt_(INv[:, 0, :], ZL[:, 0, gsl], P(20)[:, gsl], ALU.mult, w=["IN"])
                        tt_(P2[:, 4, 0:4], ZL[:, 1, gsl], P(21)[:, gsl], ALU.mult, w=["IN"])
                        tt_(INv[:, 0, :], INv[:, 0, :], P2[:, 4, 0:4], ALU.subtract, w=["IN"])
                        tt_(INv[:, 1, :], ZL[:, 0, gsl], P(21)[:, gsl], ALU.mult, w=["IN"])
                        tt_(P2[:, 4, 0:4], ZL[:, 1, gsl], P(20)[:, gsl], ALU.mult, w=["IN"])
                        tt_(INv[:, 1, :], INv[:, 1, :], P2[:, 4, 0:4], ALU.add, w=["IN"])
                    D(lambda e: e.tensor_copy(out=Rb, in_=P(3)[:, gsl].unsqueeze(2).to_broadcast([128, 4, 128])),
                      w=["Rb"])
                    D(lambda e: e.memset(Rb[:, :, 0:1], 0.0), w=["Rb"])
                    if a > 0:
                        tt_(INv, INv, P(3)[:, gsl].unsqueeze(1).to_broadcast([128, 2, 4]), ALU.mult, w=["IN"])
                        tt_(Wv[:, :, 0:512:128], Wv[:, :, 0:512:128], INv, ALU.add, w=["IN", "W0", "W1"])
                    for sidx in range(2):
                        D(lambda e: e.tensor_tensor_scan(out=Zv[:, sidx, :], data0=Rb.rearrange("p g j -> p (g j)"),
                                                         data1=Wv[:, sidx, :], initial=0.0, op0=ALU.mult, op1=ALU.add),
                          r=["Rb"], w=["Z%d" % sidx, "W%d" % sidx])
                        D(lambda e: e.tensor_copy(out=ZL[:, sidx, gsl], in_=Zv[:, sidx, 127:512:128]), w=["IN", "Z%d" % sidx])
                    tt_(Pv[:, 0, :], cosb, Zv[:, 0, :], ALU.mult, w=["Z0", "Pv"])
                    D(lambda e: e.scalar_tensor_tensor(out=Pv[:, 1, :], in0=sinb, scalar=-1.0, op0=ALU.mult,
                                                       in1=Zv[:, 1, :], op1=ALU.mult), w=["Z1", "Pv"])
                    tt_(Pv[:, 2, :], sinb, Zv[:, 0, :], ALU.mult, w=["Z0", "Pv"])
                    tt_(Pv[:, 3, :], cosb, Zv[:, 1, :], ALU.mult, w=["Z1", "Pv"])
                    by = nextbank(2, 2)
                    for i in range(4):
                        gp = 4 * q + i
                        for mi, (ci_, pi_) in enumerate(((0, 0), (0, 1), (1, 2), (1, 3))):
                            T.op("pe", lambda e: e.matmul(PS[by][:, 0:128], lhsT=CTp[:, gp, ci_, :],
                                                          rhs=Pv[:, pi_, i * 128:(i + 1) * 128],
                                                          start=(i == 0 and mi == 0), stop=(i == 3 and mi == 3)),
                                 reads=["Pv"], writes=[("ps", by)], signal=(i == 3 and mi == 3))
                    D(lambda e: e.scalar_tensor_tensor(out=YL[:, q, a * 128:(a + 1) * 128],
                                                       in0=UT[:, q, a * 128:(a + 1) * 128], scalar=DCOL[:, q:q + 1],
                                                       op0=ALU.mult, in1=PS[by][:, 0:128], op1=ALU.add),
                      w=[("ps", by), ("YL", q)])
            chk(3)
            tt_(XE[:, 0, :], ZL[:, 0, :], COS[:, :, 127], ALU.mult)
            tt_(P(13), ZL[:, 1, :], SIN[:, :, 127], ALU.mult)
            tt_(XE[:, 0, :], XE[:, 0, :], P(13), ALU.subtract)
            tt_(XE[:, 1, :], ZL[:, 0, :], SIN[:, :, 127], ALU.mult)
            tt_(P(13), ZL[:, 1, :], COS[:, :, 127], ALU.mult)
            tt_(XE[:, 1, :], XE[:, 1, :], P(13), ALU.add)
            D(lambda e: e.tensor_copy(out=GT[:, 0, :, 0:1], in_=P(7).unsqueeze(2)))
            D(lambda e: e.tensor_copy(out=GT[:, 1, :, 0:1], in_=P(8).unsqueeze(2)))
            D(lambda e: e.tensor_copy(out=P(18), in_=P(7)))
            D(lambda e: e.tensor_copy(out=P(19), in_=P(8)))
            build_table(GT[:, 0], GT[:, 1], 18, 19)
            D(lambda e: e.tensor_copy(out=P(22), in_=P(18)))
            D(lambda e: e.tensor_copy(out=P(23), in_=P(19)))
            for sidx in range(2):
                D(lambda e: e.tensor_copy(out=G[:, sidx].rearrange("p g j -> p (g j)"),
                                          in_=GT[:, sidx].rearrange("p g j -> p (g j)")), w=["PAR", "G"])
            D(lambda e: e.tensor_copy(out=P(24), in_=P(22)))
            D(lambda e: e.tensor_copy(out=P(25), in_=P(23)))
            for _ in range(3):
                cmul_par(P(28), P(29), P(24), P(25), P(24), P(25))
                D(lambda e: e.tensor_copy(out=P(24), in_=P(28)))
                D(lambda e: e.tensor_copy(out=P(25), in_=P(29)))
            chk(4)
            STG = BIGF[:, 20480:21508]
            EXb = BIGF[:, 16384:17412]
            CMo = cols[:, 304:312]
            T.dma("sp", "c0", [(CMo, cmask_d[:, :])], writes=["PAR"])
            D(lambda e: e.memset(STG, 0.0), r=["G"], w=["STG"])
            D(lambda e: e.tensor_copy(out=STG[:, 0:64], in_=XE.rearrange("p s g -> p (s g)")), w=["PAR", "STG"])
            T.dma("sp", "c2", [(cc_in[:, 0:1024], STG[:, 0:1024]), (cc_in[:, 1024:1028], STG[:, 1024:1028])],
                  reads=["STG", "PAR"], writes=["cc_in"])
            if os.environ.get("MK_SKIPCC", "0") != "1":
                T.op("pool", lambda e: e.collective_compute("AllGather", ALU.bypass, replica_groups=[list(range(CCN))],
                                                            ins=[cc_in[:, :]], outs=[cc_out[0:CCN * 128, :]]),
                     reads=["cc_in"], writes=["cc_out"])
            for j in range(CCN - 1):
                T.dma("sp", "ex", [(EXb, cc_out[j * 128:(j + 1) * 128, :])], reads=["cc_out"], writes=["EXb"])
                D(lambda e: e.tensor_copy(out=GX[:, j, :], in_=EXb[:, 0:64]), r=["EXb"], w=["PAR"])
            D(lambda e: e.memset(PAR[:, 26:28, :].rearrange("p i g -> p (i g)"), 0.0))
            for j in range(CCN - 1):
                cmul_par(P(28), P(29), P(24), P(25), P(26), P(27))
                tt_(P(28), P(28), GX[:, j, 0:32], ALU.add)
                tt_(P(29), P(29), GX[:, j, 32:64], ALU.add)
                tt_(P(28), P(28), P(26), ALU.subtract)
                tt_(P(29), P(29), P(27), ALU.subtract)
                D(lambda e: e.scalar_tensor_tensor(out=P(26), in0=P(28), scalar=CMo[:, j:j + 1], op0=ALU.mult,
                                                   in1=P(26), op1=ALU.add))
                D(lambda e: e.scalar_tensor_tensor(out=P(27), in0=P(29), scalar=CMo[:, j:j + 1], op0=ALU.mult,
                                                   in1=P(27), op1=ALU.add))
            chk(5)
            ACX = XTF[:, 6144:7168].bitcast(BF16).rearrange("p (g s c) -> p g s c", g=32, s=2)
            ACXp = BIGF[:, 20480:22528].bitcast(BF16)
            ACXp = BIGF[:, 12288:16384].bitcast(BF16).rearrange("p (g s c) -> p g s c", g=32, s=2)
            D(lambda e: e.memset(ACXp.rearrange("p g s c -> p (g s c)"), 0.0), r=["G"], w=["ACXp"])
            for a in range(NTT):
                xre = P(26).unsqueeze(2).to_broadcast([128, 32, 32])
                xim = P(27).unsqueeze(2).to_broadcast([128, 32, 32])
                AT = BIGF[:, 21504:22528].rearrange("p (g c) -> p g c", g=32)
                AT2 = XTF[:, 4096:5120].rearrange("p (g c) -> p g c", g=32)
                tt_(AT, CT[:, :, 0, :], xre, ALU.mult, w=["PAR", "AT"])
                tt_(AT2, CT[:, :, 1, :], xim, ALU.mult, w=["PAR", "AT2"])
                tt_(ACX[:, :, 0, :], AT, AT2, ALU.add, w=["PAR", "ACX", "AT", "AT2"])
                tt_(AT, CT[:, :, 0, :], xim, ALU.mult, w=["PAR", "AT"])
                tt_(AT2, CT[:, :, 1, :], xre, ALU.mult, w=["PAR", "AT2"])
                D(lambda e: e.scalar_tensor_tensor(out=ACX[:, :, 1, :], in0=AT, scalar=-1.0, op0=ALU.mult,
                                                   in1=AT2, op1=ALU.add), w=["PAR", "ACX", "AT", "AT2"])
                for i in range(4):
                    D(lambda e: e.tensor_copy(out=ACXp[:, i:32:4, :, 32 * i:32 * i + 32], in_=ACX[:, i:32:4, :, :]),
                      r=["ACX"], w=["ACXp"])
                for q in range(8):
                    by = nextbank()
                    for i in range(4):
                        gp = 4 * q + i
                        for sidx in range(2):
                            T.op("pe", lambda e: e.matmul(PS[by][:, 0:128], lhsT=ACXp[:, gp, sidx, :],
                                                          rhs=G[:, sidx, gp, :], start=(i == 0 and sidx == 0),
                                                          stop=(i == 3 and sidx == 1)),
                                 reads=["ACXp", "G"], writes=[("ps", by)], signal=(i == 3 and sidx == 1))
                    tt_(YL[:, q, a * 128:(a + 1) * 128], YL[:, q, a * 128:(a + 1) * 128], PS[by][:, 0:128], ALU.add,
                        w=[("ps", by), ("YL", q)])
                if a < NTT - 1:
                    cmul_par(P(28), P(29), P(22), P(23), P(26), P(27))
                    D(lambda e: e.tensor_copy(out=P(26), in_=P(28)))
                    D(lambda e: e.tensor_copy(out=P(27), in_=P(29)))
            T.barrier()
            chk(6)
            for q in range(8):
                T.op("act", lambda e: e.activation(out=UT[:, q, :], in_=YL[:, q, :], func=AF.Gelu), writes=[("UT", q)])
            T.barrier()
            Z2 = XT[:, 0:8, :]
            for c in range(4):
                sl = load_tile(wglu[:, :, c * 256:(c + 1) * 256], nk=8)
                for hh in range(2):
                    q = 2 * c + hh
                    for half in range(2):
                        b = nextbank()
                        for kc in range(8):
                            T.op("pe", lambda e: e.matmul(PS[b][:], lhsT=Wt[:, sl, kc, hh * 128:(hh + 1) * 128],
                                                          rhs=UT[:, kc, half * 512:(half + 1) * 512],
                                                          start=(kc == 0), stop=(kc == 7)),
                                 reads=[("w", sl)], writes=[("ps", b)], signal=(kc == 7))
                        x = (q * 2 + half) % 2
                        T.op("act", lambda e: e.activation(out=TMP[:, x, :], in_=PS[b][:], func=AF.Sigmoid,
                                                           bias=BGL[:, q:q + 1]),
                             writes=[("ps", b), ("tmp", x)])
                        T.op("dve", lambda e: e.tensor_tensor(out=Z2[:, q, half * 512:(half + 1) * 512],
                                                              in0=TMP[:, x, :], in1=UT[:, q, half * 512:(half + 1) * 512],
                                                              op=ALU.mult),
                             reads=[("tmp", x)], writes=[("Z2", q, half), "XTall"])
            T.barrier()
            chk(7)
            junk3 = TMP[:].rearrange("p a n -> p (a n)").bitcast(BF16)[:, 0:256]
            for c in range(8):
                sl = load_tile(wout[:, :, c * 256:(c + 1) * 256], nk=8)
                for tt in range(NTT):
                    b = nextbank()
                    for kc in range(8):
                        T.op("pe", lambda e: e.matmul(PS[b][:, 0:256], lhsT=Z2[:, kc, tt * 128:(tt + 1) * 128],
                                                      rhs=Wt[:, sl, kc, 0:256], start=(kc == 0), stop=(kc == 7)),
                             reads=[("w", sl), "XTall"], writes=[("ps", b)], signal=(kc == 7))
                    T.op("act", lambda e: e.activation(out=junk3, in_=PS[b][:, 0:256], func=AF.Square,
                                                       accum_out=ssq2[:, tt * 8 + c:tt * 8 + c + 1]),
                         writes=[("ps", b), "junk3", ("ssq2", tt)])
                    T.op("dve", lambda e: e.tensor_copy(out=FM[:, tt, c * 256:(c + 1) * 256], in_=PS[b][:, 0:256]),
                         writes=[("ps", b), ("F16", tt)])
            T.barrier()
            postnorm_residual(1, 3, 1.0, last=(stage_idx[0] == len(plan) - 1), Fsrc=FM, nss=8)
            T.barrier()

        stage_idx = [0]
        plan = [("ffn", 0, 0), ("even", 0), ("ffn", 0, 1), ("ffn", 1, 0), ("odd", 1), ("ffn", 1, 1)][:STAGES]
        if PLAN != "full":
            plan = [(PLAN, 0)]
        for si, st in enumerate(plan):
            stage_idx[0] = si
            if st[0] == "ffn":
                ffn(st[1], st[2])
            elif st[0] == "even":
                even_mixer()
            else:
                try:
                    odd_mixer()
                except _Stop:
                    T.barrier()
                    for tt in range(NTT):
                        T.dma("sp", "out", [(out_d[tt * 128:(tt + 1) * 128, :], H[:, tt, :])], reads=[("H", tt)])

        T.barrier()
    return nc


def _gcol(norm_g):
    g = np.asarray(norm_g, np.float32).reshape(12, KC, 128)
    return np.ascontiguousarray(g.transpose(2, 0, 1).reshape(128, 12 * KC))


def _gp_layout(a):
    a = np.asarray(a, np.float32)
    rest = a.shape[2:]
    a = a.reshape(32, 2, 64, *rest)
    a = np.moveaxis(a, 0, 2)
    return np.ascontiguousarray(a.reshape(128, 32, *rest))


def kernel(**inputs):
    x = np.asarray(inputs["x"], np.float32).reshape(SEQ, D)
    nc = build_program()
    shared = {
        "norm_g": np.ascontiguousarray(inputs["norm_g"], np.float32),
        "gcol": _gcol(inputs["norm_g"]),
        "ffn_w_gate": np.ascontiguousarray(inputs["ffn_w_gate"], np.float32),
        "ffn_w_up": np.ascontiguousarray(inputs["ffn_w_up"], np.float32),
        "ffn_w_down": np.ascontiguousarray(inputs["ffn_w_down"], np.float32),
        "ev_w_in": np.ascontiguousarray(inputs["ev_w_in"][0], np.float32),
        "ev_w_out": np.ascontiguousarray(inputs["ev_w_out"][0], np.float32),
        "ev_ln_g": np.ascontiguousarray(inputs["ev_ln_g"], np.float32).reshape(1, 1024),
        "ev_ln_b": np.ascontiguousarray(inputs["ev_ln_b"], np.float32).reshape(1, 1024),
        "ev_w_sT": np.ascontiguousarray(np.asarray(inputs["ev_w_s"][0], np.float32).transpose(2, 0, 1).reshape(128, 1024)),
        "ev_b_sT": np.ascontiguousarray(np.asarray(inputs["ev_b_s"][0], np.float32).T),
        "ev_w_gate2": np.ascontiguousarray(inputs["ev_w_gate2"][0], np.float32),
        "ev_b_gate": np.ascontiguousarray(inputs["ev_b_gate"], np.float32).reshape(1, 512),
        "ev_gla_norm_g": np.ascontiguousarray(inputs["ev_gla_norm_g"], np.float32).reshape(1, 256),
        "od_w_in": np.ascontiguousarray(inputs["od_w_in"][0], np.float32),
        "od_w_glu": np.ascontiguousarray(inputs["od_w_glu"][0], np.float32),
        "od_w_out": np.ascontiguousarray(inputs["od_w_out"][0], np.float32),
        "od_lamreT": _gp_layout(inputs["od_lam_re"][0]),
        "od_lamimT": _gp_layout(inputs["od_lam_im"][0]),
        "od_ldtT": _gp_layout(np.broadcast_to(np.asarray(inputs["od_log_dt"][0], np.float32)[:, None], (64, 64))),
        "od_breT": _gp_layout(inputs["od_b_re"][0]).reshape(128, 512),
        "od_bimT": _gp_layout(inputs["od_b_im"][0]).reshape(128, 512),
        "od_creT": _gp_layout(np.asarray(inputs["od_c_re"][0], np.float32).transpose(0, 2, 1)).reshape(128, 512),
        "od_cimT": _gp_layout(np.asarray(inputs["od_c_im"][0], np.float32).transpose(0, 2, 1)).reshape(128, 512),
        "od_dcol": np.ascontiguousarray(np.asarray(inputs["od_d"][0], np.float32).reshape(8, 128).T),
        "od_bglucol": np.ascontiguousarray(np.asarray(inputs["od_b_glu"][0], np.float32).reshape(8, 128).T),
    }
    in_maps = []
    for c in range(NCORES):
        m = dict(shared)
        m["x"] = np.ascontiguousarray(x[c * TOK:(c + 1) * TOK])
        m["cmask"] = np.ascontiguousarray(np.broadcast_to((np.arange(NCORES) < c).astype(np.float32), (128, NCORES)))
        in_maps.append(m)
    if PLAN != "full":
        for m in in_maps:
            for k in ("ffn_w_gate", "ffn_w_up", "ffn_w_down"):
                m.pop(k)
    res = run_bass_kernel_spmd(nc, in_maps, core_ids=list(range(NCORES)))
    out = np.concatenate([res.results[c]["out"] for c in range(NCORES)], axis=0)
    return out.reshape(1, SEQ, D).astype(np.float32)
```
